# Optimizing a Trainium2 kernel written in Bass

```python
import math
import jax, jax.numpy as jnp
from jax import lax
import numpy as np

D_MODEL = 1024
BATCH = 2
SEQ = 8192
DEPTH = 1

ATT_HEADS = 16
ATT_HEAD_DIM = 64
ATT_WIDTH = ATT_HEADS * ATT_HEAD_DIM
Q_BLOCK = 128
SSM_EXPAND = 2
SSM_INNER = SSM_EXPAND * D_MODEL
SSM_HEAD_DIM = 64
SSM_HEADS = SSM_INNER // SSM_HEAD_DIM
SSM_GROUPS = 4
SSM_HEADS_PER_GROUP = SSM_HEADS // SSM_GROUPS
SSM_STATE = 128
SSM_CONV = 4
SSM_CHUNK = 128
SSM_CONV_DIM = SSM_INNER + 2 * SSM_GROUPS * SSM_STATE
N_BRANCHES = 2
FFN_HIDDEN = -(-8 * D_MODEL // (3 * 256)) * 256
DEEPNORM_ALPHA = (2 * DEPTH) ** 0.25
DEEPNORM_BETA = (8 * DEPTH) ** -0.25
LN_EPS = 1e-5
RMS_EPS = 1e-5
IN_SIZES = (ATT_WIDTH, ATT_WIDTH, ATT_WIDTH, ATT_HEADS, SSM_INNER, SSM_CONV_DIM, SSM_HEADS, N_BRANCHES * D_MODEL)
IN_WIDTH = sum(IN_SIZES)

kernel_name = "fox_ssd_gated_hybrid_deepnorm"


def layer_norm(x, g, b):
    xf = x.astype(jnp.float32)
    mu = jnp.mean(xf, axis=-1, keepdims=True)
    var = jnp.mean(jnp.square(xf - mu), axis=-1, keepdims=True)
    return ((xf - mu) * lax.rsqrt(var + LN_EPS) * g.astype(jnp.float32) + b.astype(jnp.float32)).astype(x.dtype)


def forgetting_attention(q, k, v, log_f):
    bsz, seq, heads, dh = q.shape
    n_blk = seq // Q_BLOCK
    scale = 1.0 / math.sqrt(dh)
    cum = jnp.cumsum(log_f, axis=1).transpose(0, 2, 1)
    kh = k.transpose(0, 2, 1, 3)
    vh = v.transpose(0, 2, 1, 3)
    q_blocks = q.transpose(0, 2, 1, 3).reshape(bsz, heads, n_blk, Q_BLOCK, dh).transpose(2, 0, 1, 3, 4)
    dq_blocks = cum.reshape(bsz, heads, n_blk, Q_BLOCK).transpose(2, 0, 1, 3)
    key_pos = jnp.arange(seq)

    def one_block(args):
        qb, dqb, i = args
        s = jnp.einsum('bhqd,bhkd->bhqk', qb, kh, preferred_element_type=jnp.float32) * scale
        s = s + (dqb[..., :, None] - cum[..., None, :])
        q_pos = i * Q_BLOCK + jnp.arange(Q_BLOCK)
        causal = key_pos[None, :] <= q_pos[:, None]
        p = jax.nn.softmax(jnp.where(causal, s, -jnp.inf), axis=-1).astype(vh.dtype)
        return jnp.einsum('bhqk,bhkd->bhqd', p, vh)

    out = lax.map(one_block, (q_blocks, dq_blocks, jnp.arange(n_blk)))
    return out.transpose(1, 0, 3, 2, 4).reshape(bsz, seq, heads * dh)


def causal_depthwise_conv(u, w, b):
    out = lax.conv_general_dilated(u, w[:, None, :], window_strides=(1,), padding=[(SSM_CONV - 1, 0)],
                                   dimension_numbers=('NWC', 'WIO', 'NWC'), feature_group_count=u.shape[-1])
    return out + b


def ssd_chunked(x, dt, a, bmat, cmat):
    bsz, seq, heads, hd = x.shape
    nc, L = seq // SSM_CHUNK, SSM_CHUNK
    G, R, N = SSM_GROUPS, SSM_HEADS_PER_GROUP, SSM_STATE
    xc = x.reshape(bsz, nc, L, G, R, hd)
    dtc = dt.reshape(bsz, nc, L, G, R)
    bc = bmat.reshape(bsz, nc, L, G, N)
    cc = cmat.reshape(bsz, nc, L, G, N)
    da = dtc * a.reshape(G, R)
    acum = jnp.cumsum(da, axis=2).transpose(0, 1, 3, 4, 2)
    xdt = xc * dtc[..., None]
    idx = jnp.arange(L)
    causal = idx[:, None] >= idx[None, :]
    decay = jnp.exp(jnp.where(causal, acum[..., :, None] - acum[..., None, :], -jnp.inf))
    cb = jnp.einsum('bclgn,bcsgn->bcgls', cc, bc, preferred_element_type=jnp.float32)
    y_diag = jnp.einsum('bcgls,bcgrls,bcsgrp->bclgrp', cb, decay, xdt)
    decay_to_end = jnp.exp(acum[..., -1:] - acum)
    states = jnp.einsum('bcsgn,bcgrs,bcsgrp->bcgrpn', bc, decay_to_end, xdt)
    chunk_decay = jnp.exp(acum[..., -1])

    def step(h, inp):
        st, dec = inp
        return h * dec[..., None, None] + st, h

    h0 = jnp.zeros((bsz, G, R, hd, N), jnp.float32)
    _, h_prev = lax.scan(step, h0, (states.transpose(1, 0, 2, 3, 4, 5), chunk_decay.transpose(1, 0, 2, 3)))
    h_prev = h_prev.transpose(1, 0, 2, 3, 4, 5)
    y_off = jnp.einsum('bclgn,bcgrpn,bcgrl->bclgrp', cc, h_prev, jnp.exp(acum))
    return (y_diag + y_off).reshape(bsz, seq, heads, hd)


def mamba2_branch(z, xbc, dt_raw, conv_w, conv_b, dt_bias, a_log, d_skip, norm_w):
    bsz, seq, _ = z.shape
    xbc = jax.nn.silu(causal_depthwise_conv(xbc, conv_w, conv_b))
    xs, bm, cm = jnp.split(xbc, [SSM_INNER, SSM_INNER + SSM_GROUPS * SSM_STATE], axis=-1)
    xs = xs.reshape(bsz, seq, SSM_HEADS, SSM_HEAD_DIM)
    bm = bm.reshape(bsz, seq, SSM_GROUPS, SSM_STATE)
    cm = cm.reshape(bsz, seq, SSM_GROUPS, SSM_STATE)
    dt = jax.nn.softplus(dt_raw.astype(jnp.float32) + dt_bias.astype(jnp.float32))
    a = -jnp.exp(a_log.astype(jnp.float32))
    y = ssd_chunked(xs, dt, a, bm, cm) + d_skip.astype(jnp.float32)[:, None] * xs
    u = (y.reshape(bsz, seq, SSM_INNER) * jax.nn.silu(z.astype(jnp.float32))).reshape(bsz, seq, SSM_GROUPS, -1)
    u = u * lax.rsqrt(jnp.mean(jnp.square(u), axis=-1, keepdims=True) + RMS_EPS)
    return (u.reshape(bsz, seq, SSM_INNER) * norm_w.astype(jnp.float32)).astype(z.dtype)


def setup_inputs(seed: int = 0) -> dict:
    key = jax.random.key(seed)
    ks = jax.random.split(key, 20)
    f32 = jnp.float32

    def nrm(k, shape, fan_in, mult=1.0):
        return jax.random.normal(k, shape, f32) * (fan_in ** -0.5) * mult

    dt0 = jnp.exp(jax.random.uniform(ks[5], (DEPTH, SSM_HEADS), f32, math.log(1e-3), math.log(1e-1)))
    return {
        "x": jax.random.normal(ks[0], (BATCH, SEQ, D_MODEL), f32),
        "w_in": nrm(ks[1], (DEPTH, D_MODEL, IN_WIDTH), D_MODEL),
        "b_forget": jax.random.uniform(ks[2], (DEPTH, ATT_HEADS), f32, 1.0, 6.0),
        "conv_w": jax.random.uniform(ks[3], (DEPTH, SSM_CONV, SSM_CONV_DIM), f32, -0.5, 0.5),
        "conv_b": 0.02 * jax.random.normal(ks[4], (DEPTH, SSM_CONV_DIM), f32),
        "dt_bias": dt0 + jnp.log(-jnp.expm1(-dt0)),
        "a_log": jnp.log(jax.random.uniform(ks[6], (DEPTH, SSM_HEADS), f32, 1.0, 16.0)),
        "d_skip": 1.0 + 0.1 * jax.random.normal(ks[7], (DEPTH, SSM_HEADS), f32),
        "ssm_norm_w": 1.0 + 0.1 * jax.random.normal(ks[8], (DEPTH, SSM_INNER), f32),
        "w_proj_attn": nrm(ks[9], (DEPTH, ATT_WIDTH, D_MODEL), ATT_WIDTH, DEEPNORM_BETA),
        "w_proj_ssm": nrm(ks[10], (DEPTH, SSM_INNER, D_MODEL), SSM_INNER, DEEPNORM_BETA),
        "b_gates": 0.1 * jax.random.normal(ks[11], (DEPTH, N_BRANCHES * D_MODEL), f32),
        "w_out": nrm(ks[12], (DEPTH, D_MODEL, D_MODEL), D_MODEL, DEEPNORM_BETA),
        "ln1_g": 1.0 + 0.1 * jax.random.normal(ks[13], (DEPTH, D_MODEL), f32),
        "ln1_b": 0.02 * jax.random.normal(ks[14], (DEPTH, D_MODEL), f32),
        "w_ffn_gate": nrm(ks[15], (DEPTH, D_MODEL, FFN_HIDDEN), D_MODEL),
        "w_ffn_up": nrm(ks[16], (DEPTH, D_MODEL, FFN_HIDDEN), D_MODEL),
        "w_ffn_down": nrm(ks[17], (DEPTH, FFN_HIDDEN, D_MODEL), FFN_HIDDEN, DEEPNORM_BETA),
        "ln2_g": 1.0 + 0.1 * jax.random.normal(ks[18], (DEPTH, D_MODEL), f32),
        "ln2_b": 0.02 * jax.random.normal(ks[19], (DEPTH, D_MODEL), f32),
    }


def reference(x, w_in, b_forget, conv_w, conv_b, dt_bias, a_log, d_skip, ssm_norm_w, w_proj_attn,
              w_proj_ssm, b_gates, w_out, ln1_g, ln1_b, w_ffn_gate, w_ffn_up, w_ffn_down, ln2_g, ln2_b):
    bsz, seq, _ = x.shape
    split_idx = [int(i) for i in np.cumsum(IN_SIZES)[:-1]]
    for l in range(DEPTH):
        proj = x @ w_in[l]
        q, k, v, f_logit, z, xbc, dt_raw, gate_logit = jnp.split(proj, split_idx, axis=-1)
        log_f = jax.nn.log_sigmoid(f_logit.astype(jnp.float32) + b_forget[l].astype(jnp.float32))
        hs = (bsz, seq, ATT_HEADS, ATT_HEAD_DIM)
        attn = forgetting_attention(q.reshape(hs), k.reshape(hs), v.reshape(hs), log_f)
        attn_d = attn @ w_proj_attn[l]
        ssm = mamba2_branch(z, xbc, dt_raw, conv_w[l], conv_b[l], dt_bias[l], a_log[l], d_skip[l], ssm_norm_w[l])
        ssm_d = ssm @ w_proj_ssm[l]
        gates = jax.nn.sigmoid(gate_logit + b_gates[l]).reshape(bsz, seq, N_BRANCHES, D_MODEL)
        mixed = (gates[:, :, 0] * attn_d + gates[:, :, 1] * ssm_d) @ w_out[l]
        x = layer_norm(DEEPNORM_ALPHA * x + mixed, ln1_g[l], ln1_b[l])
        h = (jax.nn.silu(x @ w_ffn_gate[l]) * (x @ w_ffn_up[l])) @ w_ffn_down[l]
        x = layer_norm(DEEPNORM_ALPHA * x + h, ln2_g[l], ln2_b[l])
    return x
```

```python
import numpy as np
from contextlib import ExitStack
import concourse.bass as bass
import concourse.mybir as mybir
from concourse.bass_utils import run_bass_kernel_spmd

F32 = mybir.dt.float32
BF16 = mybir.dt.bfloat16
AF = mybir.ActivationFunctionType
ALU = mybir.AluOpType

D = 1024
NH = 16
SI = 2048
FF = 2816
NFC = 22
ALPHA = 2.0 ** 0.25
LN_EPS = 1e-5
RMS_EPS = 1e-5
NEG = -30000.0

ENGS = ("pe", "act", "dve", "pool", "sp")


class Dep:
    __slots__ = ("name", "w", "r", "dsem", "dcnt", "excl")

    def __init__(self, name, excl=False):
        self.name = name
        self.excl = excl
        self.w = []
        self.r = []
        self.dsem = None
        self.dcnt = 0


class Prog:
    def __init__(self, nc, es):
        self.nc = nc
        self.es = es
        self.h = {"pe": nc.tensor, "act": nc.scalar, "dve": nc.vector, "pool": nc.gpsimd, "sp": nc.sync}
        self.cnt = {e: 0 for e in ENGS}
        self.sem = {}
        self.nsem = 0
        for e in ENGS:
            self._newsem(e)
        self.seen = {e: {} for e in ENGS}
        self.dsems = []
        self.ninst = 0

    def _newsem(self, e):
        self.sem[e] = self.es.enter_context(self.nc.semaphore(f"s_{e}_{self.nsem}"))
        self.nsem += 1
        self.cnt[e] = 0

    def dep(self, name="d"):
        return Dep(name)

    def _waits(self, eng, reads, writes):
        evs = []
        for d in reads:
            evs.extend(d.w)
            if d.excl:
                evs.extend(e for e in d.r if e[2] != eng)
        for d in writes:
            evs.extend(d.w)
            evs.extend(d.r)
        out = {}
        seen = self.seen[eng]
        for (sem, val, src) in evs:
            if src == eng and eng == "pe":
                continue
            k = id(sem)
            if seen.get(k, 0) >= val:
                continue
            if k not in out or out[k][1] < val:
                out[k] = (sem, val)
        for k, (sem, val) in out.items():
            seen[k] = val
        return list(out.values())

    def op(self, eng, fn, r=(), w=()):
        waits = self._waits(eng, r, w)
        h = self.h[eng]
        for (s, v) in waits:
            h.wait_ge(s, v)
        ins = fn(h)
        if self.cnt[eng] >= 30000:
            self._newsem(eng)
        self.cnt[eng] += 1
        sem = self.sem[eng]
        ins.then_inc(sem, 1)
        self.ninst += 1
        ev = (sem, self.cnt[eng], eng)
        for d in r:
            d.r.append(ev)
            if len(d.r) > 24:
                d.r = d.r[-24:]
        for d in w:
            d.w = [ev]
            d.r = []
        return ev

    def dma(self, eng, out, in_, r=(), w=(), semdep=None, **kw):
        waits = self._waits(eng, r, w)
        h = self.h[eng]
        for (s, v) in waits:
            h.wait_ge(s, v)
        sd = semdep if semdep is not None else (w[0] if w else r[0])
        if sd.dsem is None:
            sd.dsem = self.es.enter_context(self.nc.semaphore(f"dm{len(self.dsems)}"))
            self.dsems.append(sd)
        sd.dcnt += 16
        h.dma_start(out=out, in_=in_, **kw).then_inc(sd.dsem, 16)
        self.ninst += 1
        ev = (sd.dsem, sd.dcnt, "dma")
        for d in r:
            d.r.append(ev)
        for d in w:
            d.w = [ev]
            d.r = []
        return ev

    def barrier(self, engs=ENGS):
        evs = [(self.sem[e], self.cnt[e]) for e in ENGS if self.cnt[e] > 0]
        evs += [(sd.dsem, sd.dcnt) for sd in self.dsems if sd.dcnt > 0]
        for e in engs:
            seen = self.seen[e]
            for (s, v) in evs:
                if s is self.sem[e]:
                    continue
                if seen.get(id(s), 0) >= v:
                    continue
                self.h[e].wait_ge(s, v)
                seen[id(s)] = v

    def mm(self, out, lhsT, rhs, start=True, stop=True, r=(), w=()):
        return self.op("pe", lambda e: e.matmul(out, lhsT=lhsT, rhs=rhs, start=start, stop=stop), r, w)

    def tr(self, out, in_, ident, r=(), w=()):
        return self.op("pe", lambda e: e.transpose(out, in_, ident), r, w)

    def act(self, out, in_, func, r=(), w=(), bias=None, scale=None, accum=None):
        kw = {}
        if bias is not None:
            kw["bias"] = bias
        if scale is not None:
            kw["scale"] = scale
        if accum is not None:
            kw["accum_out"] = accum
        return self.op("act", lambda e: e.activation(out=out, in_=in_, func=func, **kw), r, w)

    def tt(self, eng, out, in0, in1, op, r=(), w=()):
        return self.op(eng, lambda e: e.tensor_tensor(out=out, in0=in0, in1=in1, op=op), r, w)

    def ts(self, eng, out, in0, s1, op0, r=(), w=(), s2=None, op1=None):
        if op1 is None:
            return self.op(eng, lambda e: e.tensor_scalar(out=out, in0=in0, scalar1=s1, scalar2=None, op0=op0), r, w)
        return self.op(eng, lambda e: e.tensor_scalar(out=out, in0=in0, scalar1=s1, scalar2=s2, op0=op0, op1=op1), r, w)

    def stt(self, out, in0, scalar, in1, op0, op1, r=(), w=()):
        return self.op("dve", lambda e: e.scalar_tensor_tensor(out=out, in0=in0, scalar=scalar, in1=in1, op0=op0, op1=op1), r, w)

    def cp(self, eng, out, in_, r=(), w=()):
        if eng == "act":
            return self.op("act", lambda e: e.activation(out=out, in_=in_, func=AF.Copy), r, w)
        return self.op(eng, lambda e: e.tensor_copy(out=out, in_=in_), r, w)

    def memset(self, eng, ap, val, w=()):
        return self.op(eng, lambda e: e.memset(ap, val), (), w)

    def scan(self, out, d0, d1, init, op0, op1, r=(), w=()):
        return self.op("dve", lambda e: e.tensor_tensor_scan(out=out, data0=d0, data1=d1, initial=init, op0=op0, op1=op1), r, w)

    def recip(self, out, in_, r=(), w=()):
        return self.op("dve", lambda e: e.reciprocal(out=out, in_=in_), r, w)


class Sb:
    def __init__(self, big, words):
        self.big = big
        self.words = words
        self.off = 0

    def f32(self, n):
        o = self.off
        self.off += n
        assert self.off <= self.words, f"SBUF overflow {self.off} > {self.words}"
        return self.big[:, o:o + n]

    def bf16(self, n):
        assert n % 2 == 0
        o = self.off
        self.off += n // 2
        assert self.off <= self.words, f"SBUF overflow {self.off} > {self.words}"
        return self.big[:, o:o + n // 2].bitcast(BF16)

    def mark(self):
        return self.off

    def reset(self, m):
        self.off = m


def build_program(S, dbg=False):
    import os
    STOP = os.environ.get('KSTOP', 'Z')
    SKIP = os.environ.get('KSKIP', '')
    NT = S // 4
    NP = S - NT
    NS = S
    NG = NS // 512
    OG = NT // 512
    G0 = NG - OG
    NB = NS // 128
    QB0 = NP // 128
    NTT = NT // 128

    nc = bass.Bass("TRN2", target_bir_lowering=False)

    def din(name, shape, dt=F32):
        return nc.dram_tensor(name, list(shape), dt, kind="ExternalInput").ap()

    xT = din("xT", [D, NS])
    xown = din("xown", [NT, D])
    kbias_d = din("kbias", [128, NB])
    vmask_d = din("vmask", [128, NB])
    wqf_d = din("wqf", [D, NH * 65])
    wk_d = din("wk", [D, 1024])
    wv_d = din("wv", [D, 1024])
    wf_d = din("wf", [D, 16])
    wz_d = din("wz", [D, 2048])
    wxs_d = din("wxs", [D, 2048])
    wB_d = din("wB", [D, 512])
    wC_d = din("wC", [D, 512])
    wdt_d = din("wdt", [D, 32])
    wgate_d = din("wgate", [D, 2048])
    bfo_d = din("bfo", [128, 16])
    convw_d = din("convw", [128, 96])
    convb_d = din("convb", [128, 24])
    dtb_d = din("dtb", [128, 32])
    alog_d = din("alog", [128, 32])
    dskip_d = din("dskip", [128, 32])
    nw_d = din("nw", [128, 2048])
    bg_d = din("bg", [1, 2048])
    ln1g_d = din("ln1g", [128, D])
    ln1b_d = din("ln1b", [128, D])
    ln2g_d = din("ln2g", [128, D])
    ln2b_d = din("ln2b", [128, D])
    wpa_d = din("wpa", [1024, D])
    wps_d = din("wps", [2048, D])
    wo_d = din("wo", [1024, D])
    wfg_d = din("wfg", [D, FF])
    wfu_d = din("wfu", [D, FF])
    wfd_d = din("wfd", [FF, D])
    cid_d = din("c_ident", [128, 128])
    ctri_d = din("c_tri", [128, 128])
    cmask_d = din("c_mask", [128, 512])

    out_d = nc.dram_tensor("out", [NT, D], F32, kind="ExternalOutput").ap()
    dbg_d = None
    if dbg:
        dbg_attn = nc.dram_tensor("dbg_attn", [1024, NT], BF16, kind="ExternalOutput").ap()
        dbg_ssm = nc.dram_tensor("dbg_ssm", [2048, NT], BF16, kind="ExternalOutput").ap()
        dbg_x1 = nc.dram_tensor("dbg_x1", [NT, D], F32, kind="ExternalOutput").ap()

    kT_s = nc.dram_tensor("kT_s", [1024, NS], BF16, kind="Internal").ap()
    v_s = nc.dram_tensor("v_s", [NH, 128, NB, 65], BF16, kind="Internal").ap()
    qT_s = nc.dram_tensor("qT_s", [NH, 65, NT], BF16, kind="Internal").ap()
    if dbg:
        attnT_s, ssmT_s, x1_s = dbg_attn, dbg_ssm, dbg_x1
    else:
        attnT_s = nc.dram_tensor("attnT_s", [1024, NT], BF16, kind="Internal").ap()
        ssmT_s = nc.dram_tensor("ssmT_s", [2048, NT], BF16, kind="Internal").ap()
        x1_s = nc.dram_tensor("x1_s", [NT, D], F32, kind="Internal").ap()

    def kc_view(w, c0=None, c1=None):
        v = w.rearrange("(a p) n -> p a n", p=128)
        if c0 is not None:
            v = v[:, :, c0:c1]
        return v

    with ExitStack() as es:
        P = Prog(nc, es)
        BIGW = 53000
        big = es.enter_context(nc.sbuf_tensor("big", [128, BIGW], F32))
        sb = Sb(big, BIGW)
        PSB = [es.enter_context(nc.psum_tensor(f"psb{i}", [128, 512], F32)) for i in range(8)]
        PSD = [Dep(f"psb{i}", excl=True) for i in range(8)]

        def psbf(i):
            return PSB[i][:].bitcast(BF16)

        ident_bf = sb.bf16(128); d_ident = P.dep()
        tri_f = sb.f32(128); d_tri = P.dep()
        ones_f = sb.f32(128); d_ones = P.dep()
        ones5 = sb.f32(512)
        mask_bf = sb.bf16(512); d_mask = P.dep()
        bfo_bc = sb.f32(16); d_bfo = P.dep()
        nbfo_bc = sb.f32(16); d_nbfo = P.dep()
        kbias = sb.f32(NB); d_kbias = P.dep()
        vmask = sb.f32(NB); d_vmask = P.dep()
        NEGC = sb.f32(NB * 16); d_negc = P.dep()
        GPOS = sb.f32(OG * 16); d_gpos = P.dep()
        negc3 = NEGC.rearrange("p (b h) -> p b h", h=16)
        gpos3 = GPOS.rearrange("p (i h) -> p i h", h=16)

        P.dma("pool", ident_bf, cid_d, w=[d_ident])
        P.dma("sp", tri_f, ctri_d, w=[d_tri])
        P.dma("pool", mask_bf, cmask_d, w=[d_mask])
        P.dma("sp", bfo_bc, bfo_d, w=[d_bfo])
        P.dma("sp", kbias, kbias_d, w=[d_kbias])
        P.dma("sp", vmask, vmask_d, w=[d_vmask])
        P.memset("dve", ones_f, 1.0, w=[d_ones])
        P.memset("dve", ones5, 1.0, w=[d_ones])
        P.ts("dve", nbfo_bc, bfo_bc, -1.0, ALU.mult, r=[d_bfo], w=[d_nbfo])
        pmark = sb.mark()

        wk_sb = sb.bf16(8 * 1024); d_wk = P.dep()
        wv_sb = sb.bf16(8 * 1024); d_wv = P.dep()
        wf_sb = sb.bf16(8 * 16); d_wf = P.dep()
        wqf_sb = sb.bf16(8 * 1040); d_wqf = P.dep()
        wk3 = wk_sb.rearrange("p (a n) -> p a n", a=8)
        wv3 = wv_sb.rearrange("p (a n) -> p a n", a=8)
        wf3 = wf_sb.rearrange("p (a n) -> p a n", a=8)
        wqf3 = wqf_sb.rearrange("p (a n) -> p a n", a=8)
        P.dma("pool", wk3, kc_view(wk_d), w=[d_wk])
        P.dma("pool", wv3, kc_view(wv_d), w=[d_wv])
        P.dma("pool", wf3, kc_view(wf_d), w=[d_wf])
        xTb = [sb.bf16(8 * 512).rearrange("p (a n) -> p a n", a=8) for _ in range(2)]
        d_xTb = [P.dep(), P.dep()]
        KTst = [sb.bf16(8 * 512).rearrange("p (a n) -> p a n", a=8) for _ in range(2)]
        d_KTst = [P.dep(), P.dep()]
        Vst = [sb.bf16(16 * 4 * 65) for _ in range(2)]
        d_Vst = [P.dep(), P.dep()]
        Vst4 = [v.rearrange("p (h b c) -> p h b c", h=16, b=4) for v in Vst]
        QTst = sb.bf16(16 * 512).rearrange("p (h n) -> p h n", h=16); d_QTst = P.dep()
        erow = sb.f32(512); d_erow = P.dep()
        lrow = sb.f32(512); d_lrow = P.dep()
        flog = sb.f32(NB * 16); d_flog = P.dep()
        flog3 = flog.rearrange("p (b h) -> p b h", h=16)
        for i in range(2):
            P.memset("dve", Vst[i], 1.0, w=[d_Vst[i]])

        xT3 = xT.rearrange("(a p) n -> p a n", p=128)

        def load_x(gi):
            P.dma("pool", xTb[gi % 2], xT3[:, :, gi * 512:(gi + 1) * 512], w=[d_xTb[gi % 2]])

        load_x(0)
        P.dma("pool", wqf3, kc_view(wqf_d), w=[d_wqf])
        kT_s3 = kT_s.rearrange("(a p) n -> p a n", p=128)
        evk = 0
        for gi in range(NG):
            if gi + 1 < NG:
                load_x(gi + 1)
            xb = xTb[gi % 2]; dx = d_xTb[gi % 2]
            kst = KTst[gi % 2]; dk = d_KTst[gi % 2]
            for pr in range(8 if 'K' not in SKIP else 0):
                bk = pr % 2
                for kc in range(8):
                    P.mm(PSB[bk][:], wk3[:, kc, pr * 128:(pr + 1) * 128], xb[:, kc, :], start=(kc == 0), stop=(kc == 7),
                         r=[d_wk, dx], w=[PSD[bk]])
                if pr % 2 == 0:
                    P.cp("act", kst[:, pr, :], PSB[bk][:], r=[PSD[bk]], w=[dk])
                else:
                    P.cp("dve", kst[:, pr, :], PSB[bk][:], r=[PSD[bk]], w=[dk])
            if 'k' not in SKIP:
                P.dma("sp", kT_s3[:, :, gi * 512:(gi + 1) * 512], kst, r=[dk])
            vst = Vst4[gi % 2]; dv = d_Vst[gi % 2]
            for tb in range(4 if 'V' not in SKIP else 0):
                blk = gi * 4 + tb
                for half in range(2):
                    bk = 2 + half
                    for kc in range(8):
                        P.mm(PSB[bk][:], xb[:, kc, tb * 128:(tb + 1) * 128], wv3[:, kc, half * 512:(half + 1) * 512],
                             start=(kc == 0), stop=(kc == 7), r=[d_wv, dx], w=[PSD[bk]])
                    src = PSB[bk][:].rearrange("p (h c) -> p h c", h=8)
                    dst = vst[:, half * 8:(half + 1) * 8, tb, 0:64]
                    if half == 0:
                        P.cp("act", dst, src, r=[PSD[bk]], w=[dv])
                    else:
                        P.cp("dve", dst, src, r=[PSD[bk]], w=[dv])
                for kc in range(8):
                    P.mm(PSB[4][:, 0:16], xb[:, kc, tb * 128:(tb + 1) * 128], wf3[:, kc, :], start=(kc == 0), stop=(kc == 7),
                         r=[d_wf, dx], w=[PSD[4]])
                P.cp("dve", flog3[:, blk, :], PSB[4][:, 0:16], r=[PSD[4]], w=[d_flog])
            if 'v' not in SKIP:
                P.dma("sp", v_s[:, :, gi * 4:(gi + 1) * 4, :].rearrange("h p b c -> p h b c"), vst[:, :, :, 0:65], r=[dv])
            if gi >= G0 and 'Q' not in SKIP:
                I = gi - G0
                for h in range(NH):
                    bk = 5 + (h % 2)
                    for kc in range(8):
                        P.mm(PSB[bk][0:65, :], wqf3[:, kc, h * 65:(h + 1) * 65], xb[:, kc, :], start=(kc == 0), stop=(kc == 7),
                             r=[d_wqf, dx], w=[PSD[bk]])
                    P.act(QTst[0:64, h, :], PSB[bk][0:64, :], AF.Copy, r=[PSD[bk]], w=[d_QTst], scale=0.125)
                    P.act(erow[64:65, :], PSB[bk][64:65, :], AF.Exp, r=[PSD[bk], d_nbfo], w=[d_erow],
                          bias=nbfo_bc[64:65, h:h + 1], scale=-1.0)
                    P.act(lrow[64:65, :], erow[64:65, :], AF.Ln, r=[d_erow], w=[d_lrow], bias=1.0, scale=1.0)
                    P.scan(QTst[64:65, h, :], ones5[64:65, :], lrow[64:65, :], 0.0, ALU.mult, ALU.subtract,
                           r=[d_lrow, d_ones], w=[d_QTst])
                if 'q' not in SKIP:
                    P.dma("sp", qT_s[:, :, I * 512:(I + 1) * 512].rearrange("h r n -> r h n"), QTst[0:65, :, :], r=[d_QTst])

        if STOP == 'A':
            P.barrier(engs=("sp",))
            print('STOP at A', P.ninst)
            return nc
        t1 = sb.f32(NB * 16); d_t1 = P.dep()
        t2 = sb.f32(NB * 16); d_t2 = P.dep()
        t3 = sb.f32(NB * 16); d_t3 = P.dep()
        t1_3 = t1.rearrange("p (b h) -> p b h", h=16)
        t2_3 = t2.rearrange("p (b h) -> p b h", h=16)
        t3_3 = t3.rearrange("p (b h) -> p b h", h=16)
        P.tt("dve", t1_3, flog3, bfo_bc.unsqueeze(1).to_broadcast([128, NB, 16]), ALU.add, r=[d_flog, d_bfo], w=[d_t1])
        P.act(t2, t1, AF.Exp, r=[d_t1], w=[d_t2], scale=-1.0)
        P.act(t1, t2, AF.Ln, r=[d_t2], w=[d_t1], bias=1.0, scale=1.0)
        ncol = NB * 16
        for c0 in range(0, ncol, 512):
            c1 = min(ncol, c0 + 512)
            P.mm(PSB[0][:, 0:c1 - c0], tri_f, t1[:, c0:c1], r=[d_tri, d_t1], w=[PSD[0]])
            P.cp("dve", t2[:, c0:c1], PSB[0][:, 0:c1 - c0], r=[PSD[0]], w=[d_t2])
            P.mm(PSB[1][:, 0:c1 - c0], ones_f, t1[:, c0:c1], r=[d_ones, d_t1], w=[PSD[1]])
            P.cp("act", t3[:, c0:c1], PSB[1][:, 0:c1 - c0], r=[PSD[1]], w=[d_t3])
        for h in range(NH):
            P.scan(t1_3[:, :, h], ones_f[:, 0:NB], t3_3[:, :, h], 0.0, ALU.mult, ALU.add,
                   r=[d_t3, d_ones], w=[d_t1])
        for I in range(OG):
            bq = QB0 + 4 * I - 1
            if bq >= 0:
                P.cp("dve", gpos3[:, I, :], t1_3[:, bq, :], r=[d_t1], w=[d_gpos])
            else:
                P.memset("dve", gpos3[:, I, :], 0.0, w=[d_gpos])
        P.tt("dve", t2, t2, t1, ALU.add, r=[d_t1, d_t2], w=[d_t2])
        P.tt("dve", t2, t2, t3, ALU.subtract, r=[d_t3, d_t2], w=[d_t2])
        P.tt("dve", negc3, t2_3, kbias.unsqueeze(2).to_broadcast([128, NB, 16]), ALU.add, r=[d_t2, d_kbias], w=[d_negc])

        P.barrier()
        sb.reset(pmark)

        if STOP == 'A2':
            P.barrier(engs=("sp",))
            print('STOP at A2', P.ninst)
            return nc
        KTh = [sb.bf16(NS) for _ in range(2)]; d_KTh = [P.dep(), P.dep()]
        Vh = [sb.bf16(NB * 65) for _ in range(2)]; d_Vh = [P.dep(), P.dep()]
        Vh3 = [v.rearrange("p (b c) -> p b c", c=65) for v in Vh]
        QTh = [sb.bf16(NT) for _ in range(2)]; d_QTh = [P.dep(), P.dep()]
        BI = [sb.f32(OG * NB).rearrange("p (i b) -> p i b", i=OG) for _ in range(2)]; d_BI = [P.dep(), P.dep()]
        PT = [sb.bf16(512) for _ in range(3)]; d_PT = [P.dep() for _ in range(3)]
        r64 = sb.f32(512); d_r64 = P.dep()
        otsb = sb.f32(512); d_otsb = P.dep()
        ATst = [sb.bf16(512) for _ in range(2)]; d_ATst = [P.dep(), P.dep()]
        for i in range(2):
            P.memset("dve", KTh[i][64:65, :], 1.0, w=[d_KTh[i]])

        def load_head(h):
            b = h % 2
            P.dma("sp", KTh[b][0:64, :], kT_s[h * 64:(h + 1) * 64, :], w=[d_KTh[b]])
            P.dma("sp", Vh3[b], v_s[h], w=[d_Vh[b]])
            P.dma("sp", QTh[b][0:65, :], qT_s[h], w=[d_QTh[b]])

        load_head(0)
        srot = 0
        prot = 0
        tile_i = 0
        for h in range(NH):
            if h + 1 < NH:
                load_head(h + 1)
            b = h % 2
            kth = KTh[b]; vh = Vh3[b]; qth = QTh[b]
            for I in range(OG):
                P.ts("dve", BI[b][:, I, :], negc3[:, :, h], gpos3[:, I, h:h + 1], ALU.subtract, r=[d_negc, d_gpos], w=[d_BI[b]])
            tiles = []
            for I in range(OG):
                nkb = QB0 + 4 * I + 4
                ob = 3 + (tile_i % 2)
                tile_i += 1
                for j in range(nkb):
                    tiles.append((I, j, nkb, ob))

            def emit_S(n):
                I, j, nkb, ob = tiles[n]
                m = j - (QB0 + 4 * I)
                diag = m >= 0
                c0 = 128 * m if diag else 0
                sbk = n % 3
                P.mm(PSB[sbk][:, c0:512], kth[0:65, j * 128:(j + 1) * 128], qth[0:65, I * 512 + c0:(I + 1) * 512],
                     start=True, stop=(not diag), r=[d_KTh[b], d_QTh[b]], w=[PSD[sbk]])
                if diag:
                    P.mm(PSB[sbk][:, c0:c0 + 128], ident_bf, mask_bf[:, 0:128], start=False, stop=True,
                         r=[d_ident, d_mask], w=[PSD[sbk]])

            def emit_E(n):
                I, j, nkb, ob = tiles[n]
                m = j - (QB0 + 4 * I)
                diag = m >= 0
                c0 = 128 * m if diag else 0
                sbk = n % 3
                pk = n % 3
                P.act(PT[pk][:, c0:512], PSB[sbk][:, c0:512], AF.Exp, r=[PSD[sbk], d_BI[b]], w=[d_PT[pk]],
                      bias=BI[b][:, I, j:j + 1], scale=1.0)
                P.mm(PSB[ob][0:65, c0:512], vh[:, j, :], PT[pk][:, c0:512], start=(j == 0), stop=(j == nkb - 1),
                     r=[d_Vh[b], d_PT[pk]], w=[PSD[ob]])
                if j == nkb - 1:
                    P.recip(r64[64:65, :], PSB[ob][64:65, :], r=[PSD[ob]], w=[d_r64])
                    P.mm(PSB[5][0:64, :], ones_f[64:65, 0:64], r64[64:65, :], r=[d_ones, d_r64], w=[PSD[5]])
                    P.cp("act", otsb[0:64, :], PSB[ob][0:64, :], r=[PSD[ob]], w=[d_otsb])
                    ab = (h * OG + I) % 2
                    P.tt("dve", ATst[ab][0:64, :], otsb[0:64, :], PSB[5][0:64, :], ALU.mult, r=[d_otsb, PSD[5]], w=[d_ATst[ab]])
                    P.dma("sp", attnT_s[h * 64:(h + 1) * 64, I * 512:(I + 1) * 512], ATst[ab][0:64, :], r=[d_ATst[ab]])

            NTL = len(tiles)
            for n in range(min(2, NTL)):
                emit_S(n)
            for n in range(NTL):
                if n + 2 < NTL:
                    emit_S(n + 2)
                emit_E(n)

        P.barrier()
        sb.reset(pmark)

        if STOP == 'B':
            P.barrier(engs=("sp",))
            print('STOP at B', P.ninst)
            return nc
        wxs_sb = sb.bf16(8 * 2048); d_wxs = P.dep()
        wB_sb = sb.bf16(8 * 512); d_wB = P.dep()
        wC_sb = sb.bf16(8 * 512); d_wC = P.dep()
        wdt_sb = sb.bf16(8 * 32); d_wdt = P.dep()
        wxs3 = wxs_sb.rearrange("p (a n) -> p a n", a=8)
        wB3 = wB_sb.rearrange("p (a n) -> p a n", a=8)
        wC3 = wC_sb.rearrange("p (a n) -> p a n", a=8)
        wdt3 = wdt_sb.rearrange("p (a n) -> p a n", a=8)
        wz_sb = [sb.bf16(8 * 512).rearrange("p (a n) -> p a n", a=8) for _ in range(2)]
        d_wz = [P.dep(), P.dep()]
        xTb = [sb.bf16(8 * 512).rearrange("p (a n) -> p a n", a=8) for _ in range(2)]
        d_xTb = [P.dep(), P.dep()]
        load_x(0)
        P.dma("pool", wdt3, kc_view(wdt_d), w=[d_wdt])
        P.dma("pool", wxs3, kc_view(wxs_d), w=[d_wxs])
        P.dma("pool", wB3, kc_view(wB_d), w=[d_wB])
        P.dma("pool", wC3, kc_view(wC_d), w=[d_wC])
        convw = sb.f32(96); d_convw = P.dep()
        convb = sb.f32(24); d_convb = P.dep()
        dtb_bc = sb.f32(32); d_dtb = P.dep()
        a_bc = sb.f32(32); d_a = P.dep()
        dskip_bc = sb.f32(32); d_dskip = P.dep()
        nw_bc = sb.f32(2048); d_nw = P.dep()
        P.dma("sp", convw, convw_d, w=[d_convw])
        P.dma("sp", convb, convb_d, w=[d_convb])
        P.dma("sp", dtb_bc, dtb_d, w=[d_dtb])
        P.dma("sp", a_bc, alog_d, w=[d_a])
        P.dma("sp", dskip_bc, dskip_d, w=[d_dskip])
        P.dma("sp", nw_bc, nw_d, w=[d_nw])
        if 'S' not in SKIP:
            P.act(a_bc, a_bc, AF.Exp, r=[d_a], w=[d_a])
        P.ts("dve", a_bc, a_bc, -1.0, ALU.mult, r=[d_a], w=[d_a])
        convw3 = convw.rearrange("p (t k) -> p t k", k=4)
        DIAG = sb.bf16(24 * 4 * 128).rearrange("p (t k n) -> p t k n", t=24, k=4); d_diag = P.dep()
        for ct in range(24 if 'G' not in SKIP else 0):
            for k in range(4):
                P.ts("dve", DIAG[:, ct, k, :], ident_bf, convw3[:, ct, k:k + 1], ALU.mult,
                     r=[d_ident, d_convw], w=[d_diag])
        UW = 516
        Ub = [sb.bf16(6 * UW).rearrange("p (t n) -> p t n", t=6) for _ in range(2)]; d_Ub = [P.dep(), P.dep()]
        HALO = sb.bf16(4 * 6 * 4).rearrange("p (g t n) -> p g t n", g=4, t=6); d_halo = P.dep()
        SX = [sb.bf16(6 * 512).rearrange("p (t n) -> p t n", t=6) for _ in range(2)]; d_SX = [P.dep(), P.dep()]
        Hst = sb.f32(4 * 512).rearrange("p (g n) -> p g n", g=4); d_H = [P.dep() for _ in range(4)]
        hbf = sb.bf16(4 * 512).rearrange("p (g n) -> p g n", g=4); d_hbf = [P.dep() for _ in range(4)]
        if 'M' not in SKIP:
            P.memset("dve", Hst, 0.0, w=d_H)
            P.memset("dve", hbf, 0.0, w=d_hbf)
            P.memset("dve", HALO, 0.0, w=[d_halo])

        def smallf(n=128):
            return sb.f32(n), P.dep()
        dtr, d_dtr = smallf(); dte, d_dte = smallf(); da, d_da = smallf(); acum, d_acum = smallf()
        nacum, d_nacum = smallf(); tot, d_tot = smallf(); dtw, d_dtw = smallf(); cdk, d_cdk = smallf()
        eA, d_eA = smallf(); tmp, d_tmp = smallf()
        v3 = lambda t: t.rearrange("p (b r) -> p b r", r=32)
        xdtw = [sb.bf16(512) for _ in range(2)]; d_xdtw = [P.dep(), P.dep()]
        xdt = [sb.bf16(512) for _ in range(2)]; d_xdt = [P.dep(), P.dep()]
        Btok = [sb.bf16(128) for _ in range(2)]; d_Btok = [P.dep(), P.dep()]
        daU = sb.f32(1024); d_daU = P.dep()
        DEC = sb.f32(1024); d_DEC = P.dep()
        cbs = sb.f32(128); d_cbs = P.dep()
        Mt = sb.bf16(1024); d_Mt = P.dep()
        dxs = sb.f32(512); d_dxs = P.dep()
        ysb = sb.f32(512); d_ysb = P.dep()
        szs = sb.f32(512); d_szs = P.dep()
        usq = sb.f32(512); d_usq = P.dep()
        ss1 = sb.f32(2); d_ss1 = P.dep()
        obf = sb.bf16(512); d_obf = P.dep()
        SSMst = [sb.bf16(4 * 512).rearrange("p (t n) -> p t n", t=4) for _ in range(2)]; d_SSMst = [P.dep(), P.dep()]

        mmrot = 0
        cc = 0
        for gi in range(NG if 'L' not in SKIP else 0):
            own = gi >= G0
            if gi + 1 < NG:
                load_x(gi + 1)
            xb = xTb[gi % 2]; dx = d_xTb[gi % 2]
            KDT = int(os.environ.get('KDT', '99'))
            _n = [0]
            def _ok():
                _n[0] += 1
                return _n[0] <= KDT
            for tb in range(4):
                for kc in range(8):
                    P.mm(PSB[3][:, tb * 32:(tb + 1) * 32], xb[:, kc, tb * 128:(tb + 1) * 128], wdt3[:, kc, :],
                         start=(kc == 0), stop=(kc == 7), r=[d_wdt, dx], w=[PSD[3]])
            if _ok():
                P.tt("dve", v3(dtr), PSB[3][:, 0:128].rearrange("p (b r) -> p b r", r=32),
                     dtb_bc.unsqueeze(1).to_broadcast([128, 4, 32]), ALU.add, r=[PSD[3], d_dtb], w=[d_dtr])
            if _ok():
                P.act(dte, dtr, AF.Exp, r=[d_dtr], w=[d_dte])
            if _ok():
                P.act(dtr, dte, AF.Ln, r=[d_dte], w=[d_dtr], bias=1.0, scale=1.0)
            if _ok():
                P.tt("dve", v3(dte), v3(dtr), vmask[:, gi * 4:(gi + 1) * 4].unsqueeze(2).to_broadcast([128, 4, 32]), ALU.mult,
                     r=[d_dtr, d_vmask], w=[d_dte])
            if _ok():
                P.tt("dve", v3(da), v3(dte), a_bc.unsqueeze(1).to_broadcast([128, 4, 32]), ALU.mult, r=[d_dte, d_a], w=[d_da])
            if _ok():
                P.mm(PSB[3][:, 128:256], tri_f, da, r=[d_tri, d_da], w=[PSD[3]])
            if _ok():
                P.mm(PSB[3][:, 256:384], ones_f, da, r=[d_ones, d_da], w=[PSD[3]])
            if _ok():
                P.cp("dve", acum, PSB[3][:, 128:256], r=[PSD[3]], w=[d_acum])
            if _ok():
                P.cp("act", tot, PSB[3][:, 256:384], r=[PSD[3]], w=[d_tot])
            if _ok():
                P.tt("dve", tmp, tot, acum, ALU.subtract, r=[d_tot, d_acum], w=[d_tmp])
            if _ok():
                P.act(tmp, tmp, AF.Exp, r=[d_tmp], w=[d_tmp])
            if _ok():
                P.tt("dve", dtw, dte, tmp, ALU.mult, r=[d_dte, d_tmp], w=[d_dtw])
            if _ok():
                P.act(cdk, tot, AF.Exp, r=[d_tot], w=[d_cdk])
            if own:
                P.act(eA, acum, AF.Exp, r=[d_acum], w=[d_eA])
                P.ts("dve", nacum, acum, -1.0, ALU.mult, r=[d_acum], w=[d_nacum])
            for g in range(4 if 'I' not in SKIP else 0):
                ub = Ub[cc % 2]; dub = d_Ub[cc % 2]
                sx = SX[cc % 2]; dsx = d_SX[cc % 2]
                cc += 1
                needC = gi >= G0 - 1
                if 'H' not in SKIP:
                    P.cp("dve", ub[:, :, 0:3], HALO[:, g, :, 0:3], r=[d_halo], w=[dub])
                tiles = [(wxs3, d_wxs, (g * 4 + t) * 128, g * 4 + t) for t in range(4)]
                tiles.append((wB3, d_wB, g * 128, 16 + g))
                if needC:
                    tiles.append((wC3, d_wC, g * 128, 20 + g))
                for t, (w3, dw, c0, ct) in enumerate(tiles):
                    bk = mmrot % 3
                    mmrot += 1
                    for kc in range(8):
                        P.mm(PSB[bk][:], w3[:, kc, c0:c0 + 128], xb[:, kc, :], start=(kc == 0), stop=(kc == 7), r=[dw, dx], w=[PSD[bk]])
                    if t % 2 == 0:
                        P.cp("act", ub[:, t, 3:515], PSB[bk][:], r=[PSD[bk]], w=[dub])
                    else:
                        P.cp("dve", ub[:, t, 3:515], PSB[bk][:], r=[PSD[bk]], w=[dub])
                if 'H' not in SKIP:
                    P.cp("dve", HALO[:, g, :, 0:3], ub[:, :, 512:515], r=[dub], w=[d_halo])
                for t, (w3, dw, c0, ct) in enumerate(tiles):
                    if (t == 5 and not own) or 'C' in SKIP:
                        continue
                    bk = mmrot % 3
                    mmrot += 1
                    for k in range(4):
                        P.mm(PSB[bk][:], DIAG[:, ct, k, :], ub[:, t, k:k + 512], start=(k == 0), stop=(k == 3), r=[d_diag, dub], w=[PSD[bk]])
                    P.act(sx[:, t, :], PSB[bk][:], AF.Silu, r=[PSD[bk], d_convb], w=[dsx], bias=convb[:, ct:ct + 1], scale=1.0)
                if own:
                    I = gi - G0
                    wzb = wz_sb[(I * 4 + g) % 2]; dwz = d_wz[(I * 4 + g) % 2]
                    P.dma("pool", wzb, kc_view(wz_d, g * 512, (g + 1) * 512), w=[dwz])
                    sst = SSMst[(I * 4 + g) % 2]; dsst = d_SSMst[(I * 4 + g) % 2]
                for tb in range(4 if 'T' not in SKIP else 0):
                    ck = tb * 128
                    hs = slice(g * 8, g * 8 + 8)
                    q = (gi * 16 + g * 4 + tb) % 2
                    p4 = psbf(4)
                    for t in range(4):
                        P.tr(p4[:, t * 128:(t + 1) * 128], sx[:, t, ck:ck + 128], ident_bf, r=[dsx, d_ident], w=[PSD[4]])
                    P.tr(p4[:, 512:640], sx[:, 4, ck:ck + 128], ident_bf, r=[dsx, d_ident], w=[PSD[4]])
                    xs_tok = p4[:, 0:512].rearrange("p (r c) -> p r c", c=64)
                    P.cp("act", Btok[q], p4[:, 512:640], r=[PSD[4]], w=[d_Btok[q]])
                    P.tt("dve", xdtw[q].rearrange("p (r c) -> p r c", c=64), xs_tok,
                         v3(dtw)[:, tb, hs].unsqueeze(2).to_broadcast([128, 8, 64]), ALU.mult, r=[PSD[4], d_dtw], w=[d_xdtw[q]])
                    if own and 'O' not in SKIP:
                        P.tt("dve", xdt[q].rearrange("p (r c) -> p r c", c=64), xs_tok,
                             v3(dte)[:, tb, hs].unsqueeze(2).to_broadcast([128, 8, 64]), ALU.mult, r=[PSD[4], d_dte], w=[d_xdt[q]])
                        P.tt("dve", dxs.rearrange("p (r c) -> p r c", c=64), xs_tok,
                             dskip_bc[:, hs].unsqueeze(2).to_broadcast([128, 8, 64]), ALU.mult, r=[PSD[4], d_dskip], w=[d_dxs])
                        P.tt("dve", daU.rearrange("p (r l) -> p r l", r=8), tri_f.unsqueeze(1).to_broadcast([128, 8, 128]),
                             v3(da)[:, tb, hs].unsqueeze(2).to_broadcast([128, 8, 128]), ALU.mult, r=[d_tri, d_da], w=[d_daU])
                        for hf in range(2):
                            P.mm(PSB[5 + hf][:], ones_f, daU[:, hf * 512:(hf + 1) * 512], start=True, stop=False, r=[d_ones, d_daU], w=[PSD[5 + hf]])
                            P.mm(PSB[5 + hf][:], ident_bf, mask_bf, start=False, stop=True, r=[d_ident, d_mask], w=[PSD[5 + hf]])
                        for r_ in range(8):
                            P.act(DEC[:, r_ * 128:(r_ + 1) * 128], PSB[5 + r_ // 4][:, (r_ % 4) * 128:(r_ % 4 + 1) * 128], AF.Exp,
                                  r=[PSD[5 + r_ // 4], d_nacum], w=[d_DEC], bias=v3(nacum)[:, tb, g * 8 + r_:g * 8 + r_ + 1], scale=1.0)
                        P.mm(PSB[7][:, 0:128], sx[:, 4, ck:ck + 128], sx[:, 5, ck:ck + 128], r=[dsx], w=[PSD[7]])
                        P.cp("act", cbs, PSB[7][:, 0:128], r=[PSD[7]], w=[d_cbs])
                        P.tt("dve", Mt.rearrange("p (r l) -> p r l", r=8), DEC.rearrange("p (r l) -> p r l", r=8),
                             cbs.unsqueeze(1).to_broadcast([128, 8, 128]), ALU.mult, r=[d_DEC, d_cbs], w=[d_Mt])
                        for r_ in range(8):
                            P.mm(PSB[5][:, r_ * 64:(r_ + 1) * 64], Mt[:, r_ * 128:(r_ + 1) * 128], xdt[q][:, r_ * 64:(r_ + 1) * 64],
                                 r=[d_Mt, d_xdt[q]], w=[PSD[5]])
                        P.mm(PSB[6][:], sx[:, 5, ck:ck + 128], hbf[:, g, :], r=[dsx, d_hbf[g]], w=[PSD[6]])
                        P.tt("dve", ysb.rearrange("p (r c) -> p r c", c=64), PSB[6][:].rearrange("p (r c) -> p r c", c=64),
                             v3(eA)[:, tb, hs].unsqueeze(2).to_broadcast([128, 8, 64]), ALU.mult, r=[PSD[6], d_eA], w=[d_ysb])
                        P.tt("dve", ysb, ysb, PSB[5][:], ALU.add, r=[PSD[5]], w=[d_ysb])
                        P.tt("dve", ysb, ysb, dxs, ALU.add, r=[d_dxs], w=[d_ysb])
                        bk = mmrot % 3
                        mmrot += 1
                        for kc in range(8):
                            P.mm(PSB[bk][:], xb[:, kc, ck:ck + 128], wzb[:, kc, :], start=(kc == 0), stop=(kc == 7), r=[dx, dwz], w=[PSD[bk]])
                        P.act(szs, PSB[bk][:], AF.Silu, r=[PSD[bk]], w=[d_szs])
                        P.tt("dve", ysb, ysb, szs, ALU.mult, r=[d_szs], w=[d_ysb])
                        P.act(usq, ysb, AF.Square, r=[d_ysb], w=[d_usq, d_ss1], accum=ss1[:, 0:1])
                        P.act(ss1[:, 1:2], ss1[:, 0:1], AF.Sqrt, r=[d_ss1], w=[d_ss1], bias=RMS_EPS, scale=1.0 / 512.0)
                        P.recip(ss1[:, 0:1], ss1[:, 1:2], r=[d_ss1], w=[d_ss1])
                        P.stt(obf, ysb, ss1[:, 0:1], nw_bc[:, g * 512:(g + 1) * 512], ALU.mult, ALU.mult, r=[d_ysb, d_ss1, d_nw], w=[d_obf])
                        p7 = psbf(7)
                        for t in range(4):
                            P.tr(p7[:, 256 + t * 128:256 + (t + 1) * 128], obf[:, t * 128:(t + 1) * 128], ident_bf, r=[d_obf, d_ident], w=[PSD[7]])
                        P.cp("act", sst[:, :, ck:ck + 128], p7[:, 256:768].rearrange("p (t n) -> p t n", t=4), r=[PSD[7]], w=[dsst])
                    P.mm(PSB[3][:], Btok[q], xdtw[q], r=[d_Btok[q], d_xdtw[q]], w=[PSD[3]])
                    P.tt("dve", Hst[:, g, :].rearrange("p (r c) -> p r c", c=64), Hst[:, g, :].rearrange("p (r c) -> p r c", c=64),
                         v3(cdk)[:, tb, hs].unsqueeze(2).to_broadcast([128, 8, 64]), ALU.mult, r=[d_cdk], w=[d_H[g]])
                    P.tt("dve", Hst[:, g, :], Hst[:, g, :], PSB[3][:], ALU.add, r=[PSD[3]], w=[d_H[g]])
                    if gi >= G0 - 1:
                        P.cp("act", hbf[:, g, :], Hst[:, g, :], r=[d_H[g]], w=[d_hbf[g]])
                if own and 'T' not in SKIP:
                    P.dma("sp", ssmT_s[g * 512:(g + 1) * 512, I * 512:(I + 1) * 512].rearrange("(t p) n -> p t n", p=128), sst, r=[dsst])

        P.barrier()
        sb.reset(pmark)

        if STOP == 'C':
            P.barrier(engs=("sp",))
            print('STOP at C', P.ninst)
            return nc
        x1T = sb.bf16(8 * NT).rearrange("p (a n) -> p a n", a=8); d_x1T = P.dep()
        dmark = sb.mark()
        wg_sb = sb.bf16(8 * 2048).rearrange("p (a n) -> p a n", a=8); d_wg = P.dep()
        wpa_sb = sb.bf16(8 * 1024).rearrange("p (a n) -> p a n", a=8); d_wpa = P.dep()
        wps_sb = sb.bf16(16 * 1024).rearrange("p (a n) -> p a n", a=16); d_wps = P.dep()
        wo_sb = sb.bf16(8 * 1024).rearrange("p (a n) -> p a n", a=8); d_wo = P.dep()
        P.dma("pool", wg_sb, kc_view(wgate_d), w=[d_wg])
        P.dma("pool", wpa_sb, kc_view(wpa_d), w=[d_wpa])
        P.dma("pool", wps_sb, kc_view(wps_d), w=[d_wps])
        P.dma("pool", wo_sb, kc_view(wo_d), w=[d_wo])
        bg_row = sb.f32(2048); d_bg = P.dep()
        ln1g = sb.f32(D); d_ln1g = P.dep()
        ln1b = sb.f32(D); d_ln1b = P.dep()
        P.dma("sp", bg_row[0:1, :], bg_d, w=[d_bg])
        P.dma("sp", ln1g, ln1g_d, w=[d_ln1g])
        P.dma("sp", ln1b, ln1b_d, w=[d_ln1b])
        xTo = [sb.bf16(8 * 128).rearrange("p (a n) -> p a n", a=8) for _ in range(2)]; d_xTo = [P.dep(), P.dep()]
        aTt = [sb.bf16(8 * 128).rearrange("p (a n) -> p a n", a=8) for _ in range(2)]; d_aTt = [P.dep(), P.dep()]
        sTt = [sb.bf16(16 * 128).rearrange("p (a n) -> p a n", a=16) for _ in range(2)]; d_sTt = [P.dep(), P.dep()]
        xo = [sb.f32(D) for _ in range(2)]; d_xo = [P.dep(), P.dep()]
        sg = sb.f32(D); d_sg = P.dep()
        m0 = sb.f32(D); d_m0 = P.dep()
        mi = sb.bf16(D); d_mi = P.dep()
        miT = sb.bf16(8 * 128).rearrange("p (a n) -> p a n", a=8); d_miT = P.dep()
        y1 = sb.f32(D); d_y1 = P.dep()
        junk = m0; d_junk = d_m0
        st4 = sb.f32(8); d_st4 = P.dep()
        x1o = [sb.f32(D) for _ in range(2)]; d_x1o = [P.dep(), P.dep()]
        x1b = sb.bf16(D); d_x1b = P.dep()
        attnT3 = attnT_s.rearrange("(a p) n -> p a n", p=128)
        ssmT3 = ssmT_s.rearrange("(a p) n -> p a n", p=128)

        def load_tile(tt):
            b = tt % 2
            P.dma("pool", xTo[b], xT3[:, :, NP + tt * 128:NP + (tt + 1) * 128], w=[d_xTo[b]])
            P.dma("sp", aTt[b], attnT3[:, :, tt * 128:(tt + 1) * 128], w=[d_aTt[b]])
            P.dma("sp", sTt[b], ssmT3[:, :, tt * 128:(tt + 1) * 128], w=[d_sTt[b]])
            P.dma("sp", xo[b], xown[tt * 128:(tt + 1) * 128, :], w=[d_xo[b]])

        def layer_norm(src, d_src, gam, d_gam, bet, d_bet, dst, d_dst):
            P.act(junk, src, AF.Copy, r=[d_src], w=[d_junk, d_st4], accum=st4[:, 0:1])
            P.ts("dve", st4[:, 1:2], st4[:, 0:1], -1.0 / D, ALU.mult, r=[d_st4], w=[d_st4])
            P.ts("dve", src, src, st4[:, 1:2], ALU.add, r=[d_st4], w=[d_src])
            P.act(junk, src, AF.Square, r=[d_src], w=[d_junk, d_st4], accum=st4[:, 2:3])
            P.act(st4[:, 3:4], st4[:, 2:3], AF.Sqrt, r=[d_st4], w=[d_st4], bias=LN_EPS, scale=1.0 / D)
            P.recip(st4[:, 4:5], st4[:, 3:4], r=[d_st4], w=[d_st4])
            P.stt(dst, src, st4[:, 4:5], gam, ALU.mult, ALU.mult, r=[d_src, d_st4, d_gam], w=[d_dst])
            P.tt("dve", dst, dst, bet, ALU.add, r=[d_bet], w=[d_dst])

        load_tile(0)
        for tt in range(NTT):
            if tt + 1 < NTT:
                load_tile(tt + 1)
            b = tt % 2
            for br in range(2):
                for hf in range(2):
                    for kc in range(8):
                        P.mm(PSB[hf][:], xTo[b][:, kc, :], wg_sb[:, kc, br * 1024 + hf * 512: br * 1024 + (hf + 1) * 512],
                             start=(kc == 0), stop=False, r=[d_xTo[b], d_wg], w=[PSD[hf]])
                    P.mm(PSB[hf][:], ones_f[0:1, :], bg_row[0:1, br * 1024 + hf * 512: br * 1024 + (hf + 1) * 512],
                         start=False, stop=True, r=[d_ones, d_bg], w=[PSD[hf]])
                    P.act(sg[:, hf * 512:(hf + 1) * 512], PSB[hf][:], AF.Sigmoid, r=[PSD[hf]], w=[d_sg])
                for hf in range(2):
                    if br == 0:
                        for kc in range(8):
                            P.mm(PSB[2 + hf][:], aTt[b][:, kc, :], wpa_sb[:, kc, hf * 512:(hf + 1) * 512], start=(kc == 0), stop=(kc == 7),
                                 r=[d_aTt[b], d_wpa], w=[PSD[2 + hf]])
                        P.tt("dve", m0[:, hf * 512:(hf + 1) * 512], sg[:, hf * 512:(hf + 1) * 512], PSB[2 + hf][:], ALU.mult,
                             r=[d_sg, PSD[2 + hf]], w=[d_m0])
                    else:
                        for kc in range(16):
                            P.mm(PSB[2 + hf][:], sTt[b][:, kc, :], wps_sb[:, kc, hf * 512:(hf + 1) * 512], start=(kc == 0), stop=(kc == 15),
                                 r=[d_sTt[b], d_wps], w=[PSD[2 + hf]])
                        P.tt("dve", sg[:, hf * 512:(hf + 1) * 512], sg[:, hf * 512:(hf + 1) * 512], PSB[2 + hf][:], ALU.mult,
                             r=[PSD[2 + hf]], w=[d_sg])
            P.tt("dve", mi, m0, sg, ALU.add, r=[d_m0, d_sg], w=[d_mi])
            p4 = psbf(4)
            for kc in range(8):
                P.tr(p4[:, kc * 128:(kc + 1) * 128], mi[:, kc * 128:(kc + 1) * 128], ident_bf, r=[d_mi, d_ident], w=[PSD[4]])
            P.cp("act", miT, p4.rearrange("p (a n) -> p a n", a=8), r=[PSD[4]], w=[d_miT])
            for hf in range(2):
                for kc in range(8):
                    P.mm(PSB[5 + hf][:], miT[:, kc, :], wo_sb[:, kc, hf * 512:(hf + 1) * 512], start=(kc == 0), stop=(kc == 7),
                         r=[d_miT, d_wo], w=[PSD[5 + hf]])
                P.stt(y1[:, hf * 512:(hf + 1) * 512], xo[b][:, hf * 512:(hf + 1) * 512], ALPHA, PSB[5 + hf][:], ALU.mult, ALU.add,
                      r=[d_xo[b], PSD[5 + hf]], w=[d_y1])
            layer_norm(y1, d_y1, ln1g, d_ln1g, ln1b, d_ln1b, x1o[b], d_x1o[b])
            P.dma("sp", x1_s[tt * 128:(tt + 1) * 128, :], x1o[b], r=[d_x1o[b]])
            P.cp("dve", x1b, x1o[b], r=[d_x1o[b]], w=[d_x1b])
            p7 = psbf(7)
            for kc in range(8):
                P.tr(p7[:, kc * 128:(kc + 1) * 128], x1b[:, kc * 128:(kc + 1) * 128], ident_bf, r=[d_x1b, d_ident], w=[PSD[7]])
            P.cp("act", x1T[:, :, tt * 128:(tt + 1) * 128], p7.rearrange("p (a n) -> p a n", a=8), r=[PSD[7]], w=[d_x1T])

        P.barrier()
        sb.reset(dmark)
        if STOP == 'D1':
            P.barrier(engs=("sp",))
            print('STOP at D1', P.ninst)
            return nc
        HM = sb.bf16(NFC * NT).rearrange("p (c n) -> p c n", c=NFC); d_HM = P.dep()
        wd_sb = sb.bf16(NFC * 1024).rearrange("p (c n) -> p c n", c=NFC); d_wd = P.dep()
        ln2g = sb.f32(D); d_ln2g = P.dep()
        ln2b = sb.f32(D); d_ln2b = P.dep()
        P.dma("sp", ln2g, ln2g_d, w=[d_ln2g])
        P.dma("sp", ln2b, ln2b_d, w=[d_ln2b])
        fmark = sb.mark()
        wgc = [sb.bf16(8 * 128).rearrange("p (a n) -> p a n", a=8) for _ in range(2)]; d_wgc = [P.dep(), P.dep()]
        wuc = [sb.bf16(8 * 128).rearrange("p (a n) -> p a n", a=8) for _ in range(2)]; d_wuc = [P.dep(), P.dep()]
        sgf = [sb.f32(512) for _ in range(2)]; d_sgf = [P.dep(), P.dep()]

        def load_ffc(c):
            P.dma("pool", wgc[c % 2], kc_view(wfg_d, c * 128, (c + 1) * 128), w=[d_wgc[c % 2]])
            P.dma("pool", wuc[c % 2], kc_view(wfu_d, c * 128, (c + 1) * 128), w=[d_wuc[c % 2]])

        load_ffc(0)
        k = 0
        for c in range(NFC):
            if c + 1 < NFC:
                load_ffc(c + 1)
            if c == 1:
                P.dma("pool", wd_sb, wfd_d.rearrange("(c p) n -> p c n", p=128), w=[d_wd])
            for tg in range(NT // 512):
                bg_ = (k % 2) * 2
                q = k % 2
                k += 1
                for kc in range(8):
                    P.mm(PSB[bg_][:], wgc[c % 2][:, kc, :], x1T[:, kc, tg * 512:(tg + 1) * 512], start=(kc == 0), stop=(kc == 7),
                         r=[d_wgc[c % 2], d_x1T], w=[PSD[bg_]])
                for kc in range(8):
                    P.mm(PSB[bg_ + 1][:], wuc[c % 2][:, kc, :], x1T[:, kc, tg * 512:(tg + 1) * 512], start=(kc == 0), stop=(kc == 7),
                         r=[d_wuc[c % 2], d_x1T], w=[PSD[bg_ + 1]])
                P.act(sgf[q], PSB[bg_][:], AF.Silu, r=[PSD[bg_]], w=[d_sgf[q]])
                P.tt("dve", HM[:, c, tg * 512:(tg + 1) * 512], sgf[q], PSB[bg_ + 1][:], ALU.mult, r=[d_sgf[q], PSD[bg_ + 1]], w=[d_HM])
        P.barrier()
        sb.reset(fmark)
        x1r = [sb.f32(D) for _ in range(2)]; d_x1r = [P.dep(), P.dep()]
        y2 = sb.f32(D); d_y2 = P.dep()
        junk = sb.f32(D); d_junk = P.dep()
        st4 = sb.f32(8); d_st4 = P.dep()
        oo = [sb.f32(D) for _ in range(2)]; d_oo = [P.dep(), P.dep()]
        for tt in range(NTT):
            b = tt % 2
            P.dma("sp", x1r[b], x1_s[tt * 128:(tt + 1) * 128, :], w=[d_x1r[b]])
            for hf in range(2):
                bk = 4 + b * 2 + hf
                for c in range(NFC):
                    P.mm(PSB[bk][:], HM[:, c, tt * 128:(tt + 1) * 128], wd_sb[:, c, hf * 512:(hf + 1) * 512], start=(c == 0), stop=(c == NFC - 1),
                         r=[d_HM, d_wd], w=[PSD[bk]])
                P.stt(y2[:, hf * 512:(hf + 1) * 512], x1r[b][:, hf * 512:(hf + 1) * 512], ALPHA, PSB[bk][:], ALU.mult, ALU.add,
                      r=[d_x1r[b], PSD[bk]], w=[d_y2])
            layer_norm(y2, d_y2, ln2g, d_ln2g, ln2b, d_ln2b, oo[b], d_oo[b])
            P.dma("sp", out_d[tt * 128:(tt + 1) * 128, :], oo[b], r=[d_oo[b]])
        P.barrier(engs=("sp",))
        print("program built: instructions", P.ninst, "dma sems", len(P.dsems), "eng sems", P.nsem)
    return nc


_CACHE = {}


def _consts():
    ident = np.eye(128, dtype=np.float32)
    k = np.arange(128)
    tri = (k[:, None] <= k[None, :]).astype(np.float32)
    m = np.where(k[:, None] > k[None, :], NEG, 0.0).astype(np.float32)
    return ident, tri, np.tile(m, (1, 4))


def prep_inputs(S, x, w_in, b_forget, conv_w, conv_b, dt_bias, a_log, d_skip, ssm_norm_w, w_proj_attn, w_proj_ssm,
                b_gates, w_out, ln1_g, ln1_b, w_ffn_gate, w_ffn_up, w_ffn_down, ln2_g, ln2_b):
    f = lambda a: np.ascontiguousarray(np.asarray(a, dtype=np.float32))
    NT = S // 4
    NP = S - NT
    NB = S // 128
    W = f(w_in[0])
    wq, wk, wv, wf = W[:, 0:1024], W[:, 1024:2048], W[:, 2048:3072], W[:, 3072:3088]
    wz = W[:, 3088:5136]
    wxbc = W[:, 5136:8208]
    wdt = W[:, 8208:8240]
    wgate = W[:, 8240:10288]
    wqf = np.concatenate([wq.reshape(1024, 16, 64), wf.reshape(1024, 16, 1)], axis=2).reshape(1024, 16 * 65)
    bc = lambda v: np.ascontiguousarray(np.broadcast_to(f(v).reshape(1, -1), (128, f(v).size)))
    cw = f(conv_w[0])
    convw = np.ascontiguousarray(cw.reshape(4, 24, 128).transpose(2, 1, 0)).reshape(128, 96)
    convb = np.ascontiguousarray(f(conv_b[0]).reshape(24, 128).T)
    ident, tri, mask4 = _consts()
    shared = dict(
        wqf=f(wqf), wk=f(wk), wv=f(wv), wf=f(wf), wz=f(wz), wxs=f(wxbc[:, 0:2048]), wB=f(wxbc[:, 2048:2560]), wC=f(wxbc[:, 2560:3072]),
        wdt=f(wdt), wgate=f(wgate), bfo=bc(b_forget[0]), convw=convw, convb=convb, dtb=bc(dt_bias[0]), alog=bc(a_log[0]),
        dskip=bc(d_skip[0]), nw=bc(ssm_norm_w[0]), bg=f(b_gates[0]).reshape(1, 2048), ln1g=bc(ln1_g[0]), ln1b=bc(ln1_b[0]),
        ln2g=bc(ln2_g[0]), ln2b=bc(ln2_b[0]), wpa=f(w_proj_attn[0]), wps=f(w_proj_ssm[0]), wo=f(w_out[0]),
        wfg=f(w_ffn_gate[0]), wfu=f(w_ffn_up[0]), wfd=f(w_ffn_down[0]), c_ident=ident, c_tri=tri, c_mask=mask4,
    )
    x = np.asarray(x, dtype=np.float32)
    in_maps = []
    for c in range(8):
        b, k = c // 4, c % 4
        npad = NP - k * NT
        xT = np.zeros((D, S), np.float32)
        xT[:, npad:] = x[b, :(k + 1) * NT, :].T
        valid = (np.arange(S) >= npad)
        kb = np.where(valid, 0.0, NEG).astype(np.float32).reshape(NB, 128).T
        vm = valid.astype(np.float32).reshape(NB, 128).T
        m = dict(shared)
        m.update(xT=xT, xown=np.ascontiguousarray(x[b, k * NT:(k + 1) * NT, :]), kbias=np.ascontiguousarray(kb), vmask=np.ascontiguousarray(vm))
        in_maps.append(m)
    return in_maps


def run(S, inputs, dbg=False, trace=False):
    key = (S, dbg)
    if key not in _CACHE:
        _CACHE[key] = build_program(S, dbg)
    nc = _CACHE[key]
    in_maps = prep_inputs(S, **inputs)
    res = run_bass_kernel_spmd(nc, in_maps, core_ids=list(range(8)), **({"trace": True} if trace else {}))
    return res


def kernel(**inputs):
    x = np.asarray(inputs["x"])
    B, S, _ = x.shape
    res = run(S, inputs)
    NT = S // 4
    out = np.zeros((B, S, D), np.float32)
    for c in range(8):
        b, k = c // 4, c % 4
        out[b, k * NT:(k + 1) * NT, :] = np.asarray(res.results[c]["out"], dtype=np.float32)
    return out
```
